# Optimizing a Trainium2 kernel written in Bass

```python
import jax, jax.numpy as jnp
from jax import lax
import numpy as np

D_MODEL = 4096
BATCH = 2
SEQ = 4096
DEPTH = 2

CHUNK = 64
MIX_WIDTH = D_MODEL
SGU_BLOCK = 128
SGU_WIDTH = MIX_WIDTH // 4
SGU_HEAD_DIM = 128
SGU_HEADS = SGU_WIDTH // SGU_HEAD_DIM
MLA_NOPE_DIM = 128
MLA_ROPE_DIM = 64
MLA_V_DIM = 128
MLA_WIDTH = MIX_WIDTH // 2
MLA_HEADS = MLA_WIDTH // MLA_V_DIM
MLA_Q_RANK = D_MODEL // 4
MLA_KV_RANK = D_MODEL // 8
Q_BLOCK = 128
RET_WIDTH = MIX_WIDTH // 4
RET_V_DIM = 128
RET_HEADS = RET_WIDTH // RET_V_DIM
RET_QK_DIM = RET_V_DIM // 2
FF_DIM = 4 * D_MODEL
ROPE_THETA = 10000.0
NORM_EPS = 1e-6

IN_SIZES = (SGU_WIDTH, SGU_WIDTH,
            MLA_Q_RANK, MLA_KV_RANK, MLA_ROPE_DIM,
            RET_HEADS * RET_QK_DIM, RET_HEADS * RET_QK_DIM, RET_WIDTH, RET_WIDTH)
IN_WIDTH = sum(IN_SIZES)

kernel_name = "hybrid_sgu_mla_retention_sandwich"


def rms_norm(x, g):
    xf = x.astype(jnp.float32)
    y = xf * lax.rsqrt(jnp.mean(xf * xf, axis=-1, keepdims=True) + NORM_EPS)
    return (y * g.astype(jnp.float32)).astype(x.dtype)


def layer_norm(x, g, b):
    xf = x.astype(jnp.float32)
    mu = jnp.mean(xf, axis=-1, keepdims=True)
    xc = xf - mu
    y = xc * lax.rsqrt(jnp.mean(xc * xc, axis=-1, keepdims=True) + NORM_EPS)
    return (y * g.astype(jnp.float32) + b.astype(jnp.float32)).astype(x.dtype)


def rope_tables(seq_len, dim):
    inv_freq = 1.0 / (ROPE_THETA ** (jnp.arange(0, dim, 2, dtype=jnp.float32) / dim))
    ang = jnp.arange(seq_len, dtype=jnp.float32)[:, None] * inv_freq[None, :]
    return jnp.cos(ang), jnp.sin(ang)


def apply_rope(x, cos, sin):
    x1, x2 = jnp.split(x, 2, axis=-1)
    c = cos[None, :, None, :].astype(x.dtype)
    s = sin[None, :, None, :].astype(x.dtype)
    return jnp.concatenate([x1 * c - x2 * s, x1 * s + x2 * c], axis=-1)


def sgu_mixer(u, v, ln_g, ln_b, w_s, b_s):
    bsz, s_len, _ = u.shape
    v = layer_norm(v.reshape(bsz, s_len, SGU_HEADS, SGU_HEAD_DIM),
                   ln_g.reshape(SGU_HEADS, SGU_HEAD_DIM), ln_b.reshape(SGU_HEADS, SGU_HEAD_DIM))
    v = v.reshape(bsz, s_len // SGU_BLOCK, SGU_BLOCK, SGU_HEADS, SGU_HEAD_DIM)
    pos_chunk = jnp.arange(SGU_BLOCK) // CHUNK
    mask = pos_chunk[None, :] <= pos_chunk[:, None]
    w = jnp.where(mask[None], w_s, 0.0).astype(v.dtype)
    mixed = jnp.einsum('gij,bnjgc->bnigc', w, v) + b_s.T[None, None, :, :, None].astype(v.dtype)
    return u * mixed.reshape(bsz, s_len, SGU_WIDTH)


def mla_mixer(c_q, c_kv, k_rope, q_norm, wq_b, kv_norm, wkv_b, cos, sin):
    bsz, s_len, _ = c_q.shape
    q = (rms_norm(c_q, q_norm) @ wq_b).reshape(bsz, s_len, MLA_HEADS, MLA_NOPE_DIM + MLA_ROPE_DIM)
    q_nope = q[..., :MLA_NOPE_DIM]
    q_rope = apply_rope(q[..., MLA_NOPE_DIM:], cos, sin)
    kv = (rms_norm(c_kv, kv_norm) @ wkv_b).reshape(bsz, s_len, MLA_HEADS, MLA_NOPE_DIM + MLA_V_DIM)
    k_nope = kv[..., :MLA_NOPE_DIM]
    val = kv[..., MLA_NOPE_DIM:]
    k_rope = apply_rope(k_rope[:, :, None, :], cos, sin)[:, :, 0, :]
    scale = (MLA_NOPE_DIM + MLA_ROPE_DIM) ** -0.5
    n_blk = s_len // Q_BLOCK
    qn_blocks = q_nope.reshape(bsz, n_blk, Q_BLOCK, MLA_HEADS, MLA_NOPE_DIM).transpose(1, 0, 2, 3, 4)
    qr_blocks = q_rope.reshape(bsz, n_blk, Q_BLOCK, MLA_HEADS, MLA_ROPE_DIM).transpose(1, 0, 2, 3, 4)
    k_chunk = jnp.arange(s_len) // CHUNK

    def attend(args):
        qn, qr, blk = args
        s = (jnp.einsum('bqhd,bkhd->bhqk', qn, k_nope)
             + jnp.einsum('bqhr,bkr->bhqk', qr, k_rope)).astype(jnp.float32) * scale
        q_chunk = (blk * Q_BLOCK + jnp.arange(Q_BLOCK)) // CHUNK
        mask = k_chunk[None, :] <= q_chunk[:, None]
        s = jnp.where(mask[None, None], s, -jnp.inf)
        p = jax.nn.softmax(s, axis=-1).astype(val.dtype)
        return jnp.einsum('bhqk,bkhd->bqhd', p, val)

    out = lax.map(attend, (qn_blocks, qr_blocks, jnp.arange(n_blk)))
    return out.transpose(1, 0, 2, 3, 4).reshape(bsz, s_len, MLA_WIDTH)


def retention_mixer(q, k, v, gate, cos, sin):
    bsz, s_len, _ = q.shape
    n_chunk = s_len // CHUNK
    dt = q.dtype
    q = apply_rope(q.reshape(bsz, s_len, RET_HEADS, RET_QK_DIM), cos, sin) * (RET_QK_DIM ** -0.5)
    k = apply_rope(k.reshape(bsz, s_len, RET_HEADS, RET_QK_DIM), cos, sin)
    qc = q.reshape(bsz, n_chunk, CHUNK, RET_HEADS, RET_QK_DIM)
    kc = k.reshape(bsz, n_chunk, CHUNK, RET_HEADS, RET_QK_DIM)
    vc = v.reshape(bsz, n_chunk, CHUNK, RET_HEADS, RET_V_DIM)
    log_gamma = jnp.log1p(-jnp.exp2(-5.0 - jnp.arange(RET_HEADS, dtype=jnp.float32)))
    idx = jnp.arange(CHUNK, dtype=jnp.float32)
    intra_decay = jnp.exp(log_gamma[:, None, None] * jnp.abs(idx[:, None] - idx[None, :]))
    q_decay = jnp.exp(log_gamma[None, :] * (idx[:, None] + 1.0))
    k_decay = jnp.exp(log_gamma[None, :] * (CHUNK - 1.0 - idx[:, None]))
    chunk_decay = jnp.exp(log_gamma * CHUNK)
    scores = jnp.einsum('bclhd,bcmhd->bchlm', qc, kc) * intra_decay.astype(dt)
    intra_out = jnp.einsum('bchlm,bcmhe->bclhe', scores, vc)
    kv_chunks = jnp.einsum('bclhd,bclhe->cbhde', kc * k_decay[None, None, :, :, None].astype(dt), vc)

    def step(state, kv_c):
        return chunk_decay[None, :, None, None] * state + kv_c, state

    init = jnp.zeros((bsz, RET_HEADS, RET_QK_DIM, RET_V_DIM), jnp.float32)
    _, prev_states = lax.scan(step, init, kv_chunks.astype(jnp.float32))
    cross_out = jnp.einsum('bclhd,cbhde->bclhe', qc * q_decay[None, None, :, :, None].astype(dt),
                           prev_states.astype(dt))
    o = (intra_out + cross_out).reshape(bsz, s_len, RET_HEADS, RET_V_DIM).astype(jnp.float32)
    o = o * lax.rsqrt(jnp.mean(o * o, axis=-1, keepdims=True) + NORM_EPS)
    return jax.nn.silu(gate) * o.reshape(bsz, s_len, RET_WIDTH).astype(dt)


def hybrid_layer(x, n_mix_pre, n_mix_post, n_ffn_pre, n_ffn_post, w_in,
                 sgu_ln_g, sgu_ln_b, sgu_w, sgu_b,
                 mla_q_norm, mla_wq_b, mla_kv_norm, mla_wkv_b,
                 w_out, w_up, w_down, cos, sin):
    h = rms_norm(x, n_mix_pre)
    proj = h @ w_in
    offsets = [int(o) for o in np.cumsum(IN_SIZES)[:-1]]
    u, v, c_q, c_kv, k_rope, r_q, r_k, r_v, r_g = jnp.split(proj, offsets, axis=-1)
    out_a = sgu_mixer(u, v, sgu_ln_g, sgu_ln_b, sgu_w, sgu_b)
    out_b = mla_mixer(c_q, c_kv, k_rope, mla_q_norm, mla_wq_b, mla_kv_norm, mla_wkv_b, cos, sin)
    out_c = retention_mixer(r_q, r_k, r_v, r_g, cos, sin)
    mixed = jnp.concatenate([out_a, out_b, out_c], axis=-1) @ w_out
    x = x + rms_norm(mixed, n_mix_post)
    h = rms_norm(x, n_ffn_pre)
    f = jnp.square(jax.nn.relu(h @ w_up)) @ w_down
    return x + rms_norm(f, n_ffn_post)


def setup_inputs(seed: int = 0) -> dict:
    key = jax.random.key(seed)
    ks = jax.random.split(key, 20)

    def nrm(k, shape, scale):
        return jax.random.normal(k, shape, jnp.float32) * scale

    L = DEPTH
    return {
        "x": nrm(ks[0], (BATCH, SEQ, D_MODEL), 1.0),
        "norm_mix_pre": 1.0 + nrm(ks[1], (L, D_MODEL), 0.02),
        "norm_mix_post": 1.0 + nrm(ks[2], (L, D_MODEL), 0.02),
        "norm_ffn_pre": 1.0 + nrm(ks[3], (L, D_MODEL), 0.02),
        "norm_ffn_post": 1.0 + nrm(ks[4], (L, D_MODEL), 0.02),
        "w_in": nrm(ks[5], (L, D_MODEL, IN_WIDTH), D_MODEL ** -0.5),
        "sgu_ln_g": 1.0 + nrm(ks[6], (L, SGU_WIDTH), 0.02),
        "sgu_ln_b": nrm(ks[7], (L, SGU_WIDTH), 0.02),
        "sgu_w": nrm(ks[8], (L, SGU_HEADS, SGU_BLOCK, SGU_BLOCK), SGU_BLOCK ** -0.5),
        "sgu_b": 1.0 + nrm(ks[9], (L, SGU_HEADS, SGU_BLOCK), 0.1),
        "mla_q_norm": 1.0 + nrm(ks[10], (L, MLA_Q_RANK), 0.02),
        "mla_wq_b": nrm(ks[11], (L, MLA_Q_RANK, MLA_HEADS * (MLA_NOPE_DIM + MLA_ROPE_DIM)), MLA_Q_RANK ** -0.5),
        "mla_kv_norm": 1.0 + nrm(ks[12], (L, MLA_KV_RANK), 0.02),
        "mla_wkv_b": nrm(ks[13], (L, MLA_KV_RANK, MLA_HEADS * (MLA_NOPE_DIM + MLA_V_DIM)), MLA_KV_RANK ** -0.5),
        "w_out": nrm(ks[14], (L, MIX_WIDTH, D_MODEL), MIX_WIDTH ** -0.5),
        "w_up": nrm(ks[15], (L, D_MODEL, FF_DIM), D_MODEL ** -0.5),
        "w_down": nrm(ks[16], (L, FF_DIM, D_MODEL), FF_DIM ** -0.5),
    }


def reference(x, norm_mix_pre, norm_mix_post, norm_ffn_pre, norm_ffn_post, w_in,
              sgu_ln_g, sgu_ln_b, sgu_w, sgu_b,
              mla_q_norm, mla_wq_b, mla_kv_norm, mla_wkv_b,
              w_out, w_up, w_down):
    cos, sin = rope_tables(x.shape[1], MLA_ROPE_DIM)
    for l in range(DEPTH):
        x = hybrid_layer(x, norm_mix_pre[l], norm_mix_post[l], norm_ffn_pre[l], norm_ffn_post[l], w_in[l],
                         sgu_ln_g[l], sgu_ln_b[l], sgu_w[l], sgu_b[l],
                         mla_q_norm[l], mla_wq_b[l], mla_kv_norm[l], mla_wkv_b[l],
                         w_out[l], w_up[l], w_down[l], cos, sin)
    return x
```

```python
import numpy as np
import ml_dtypes
import concourse.bass as bass
import concourse.mybir as mybir
from concourse.bass_utils import run_bass_kernel_spmd

F32 = mybir.dt.float32
BF16 = mybir.dt.bfloat16
AF = mybir.ActivationFunctionType
ALU = mybir.AluOpType
AX = mybir.AxisListType

NCORES = 8
D = 4096
TOK = 1024
T = 512
NT = TOK // T
SEQ = 4096
DEPTH = 2
FF = 16384
INW = 6720
EPS = 1e-6
O_U, O_V, O_CQ, O_CKV, O_KR, O_RQ, O_RK, O_RV, O_RG = 0, 1024, 2048, 3072, 3584, 3648, 4160, 4672, 5696
WSPEC = [("w_in", 4096, 6720), ("wq_b", 1024, 3072), ("wkv_b", 512, 4096),
         ("w_out", 4096, 4096), ("w_up", 4096, 16384), ("w_down", 16384, 4096)]
KT_ROWS = 2048 + 64 + 512
V_ROWS = 128 * 24
KV_ROWS = KT_ROWS + V_ROWS
ATT_SCALE = float((128 + 64) ** -0.5)
RET_SCALE = float(64 ** -0.5)


class Res:
    __slots__ = ("name", "w", "rc", "rd", "excl")

    def __init__(self, name, excl=False):
        self.name = name
        self.excl = excl
        self.w = None
        self.rc = {}
        self.rd = []


class Op:
    __slots__ = ("eng", "fn", "deps", "dma", "coll", "sem", "val", "inc", "users")


class _Rec:
    def __init__(self):
        self.call = None

    def __getattr__(self, name):
        def f(*a, **kw):
            self.call = (name, a, kw)
            return None
        return f


class Sched:
    ENGS = ("pe", "act", "dve", "pool", "sp")

    def __init__(self, nc, ndma=10):
        self.nc = nc
        self.q = {e: [] for e in self.ENGS}
        self.fence_deps = []
        self.lastop = {e: None for e in self.ENGS}
        self.dmas_since_fence = []
        self.ndma = ndma
        self.dma_slot_last = {e: [None] * ndma for e in self.ENGS}
        self.dma_slot_cnt = {e: [0] * ndma for e in self.ENGS}
        self.dma_rr = {e: 0 for e in self.ENGS}
        self.colls = []
        self.nops_since_fence = 0

    def add(self, eng, fn, reads=(), writes=(), dma=False, coll=False):
        op = Op()
        rec = _Rec()
        fn(rec)
        assert rec.call is not None
        op.eng, op.fn, op.dma, op.coll = eng, rec.call, dma, coll
        op.inc = dma or coll
        op.users = 0
        op.sem = None
        op.val = None
        deps = set(self.fence_deps)
        for r in reads:
            if r.w is not None:
                deps.add(r.w)
            if r.excl:
                for e_, o_ in r.rc.items():
                    if e_ != eng:
                        deps.add(o_)
        for w in writes:
            if w.w is not None:
                deps.add(w.w)
            deps.update(w.rc.values())
            deps.update(w.rd)
        if dma:
            k = self.dma_rr[eng]
            self.dma_rr[eng] = (k + 1) % self.ndma
            prev = self.dma_slot_last[eng][k]
            if prev is not None:
                deps.add(prev)
            self.dma_slot_last[eng][k] = op
            self.dma_slot_cnt[eng][k] += 1
            op.sem = ("dma", eng, k)
            op.val = 16 * self.dma_slot_cnt[eng][k]
            self.dmas_since_fence.append(op)
        if coll:
            op.sem = ("coll", len(self.colls))
            op.val = 1
            self.colls.append(op)
            self.dmas_since_fence.append(op)
        async_op = dma or coll
        fdeps = set()
        for d in deps:
            if d is op:
                continue
            if (not async_op) and (not d.dma) and (not d.coll) and d.eng == eng and eng == "pe":
                continue
            fdeps.add(d)
        op.deps = fdeps
        for d in fdeps:
            d.users += 1
        for r in reads:
            if async_op:
                r.rd.append(op)
            else:
                r.rc[eng] = op
        for w in writes:
            w.w = op
            w.rc = {}
            w.rd = []
        self.q[eng].append(op)
        self.nops_since_fence += 1
        if not async_op:
            self.lastop[eng] = op
        return op

    def fence(self):
        if self.nops_since_fence == 0:
            return
        deps = [o for o in self.lastop.values() if o is not None]
        deps += self.dmas_since_fence
        self.dmas_since_fence = []
        self.fence_deps = deps
        self.nops_since_fence = 0

    def emit(self, final_waits):
        nc = self.nc
        for e in self.ENGS:
            cnt = 0
            for op in self.q[e]:
                if op.dma or op.coll:
                    continue
                if op.users > 0:
                    cnt += 1
                    op.inc = True
                    op.sem = ("eng", e)
                    op.val = cnt
        names = set()
        for e in self.ENGS:
            for op in self.q[e]:
                if op.sem is not None:
                    names.add(op.sem)
        names = sorted(names, key=str)
        import contextlib
        with contextlib.ExitStack() as st:
            sems = {}
            for i, n in enumerate(names):
                sems[n] = st.enter_context(nc.semaphore("s%d" % i))
            block = st.enter_context(nc.Block())

            def run(eng_name):
                def body(eng):
                    waited = {}
                    for op in self.q[eng_name]:
                        for d in sorted(op.deps, key=lambda d: (str(d.sem), d.val)):
                            if waited.get(d.sem, 0) < d.val:
                                eng.wait_ge(sems[d.sem], d.val)
                                waited[d.sem] = d.val
                        name_, a_, kw_ = op.fn
                        ins = getattr(eng, name_)(*a_, **kw_)
                        if op.coll:
                            ins.then_inc(sems[op.sem])
                        elif op.dma:
                            ins.then_inc(sems[op.sem], 16)
                        elif op.inc:
                            ins.then_inc(sems[op.sem], 1)
                    if eng_name == "sp":
                        for d in final_waits:
                            if waited.get(d.sem, 0) < d.val:
                                eng.wait_ge(sems[d.sem], d.val)
                                waited[d.sem] = d.val
                return body

            block.tensor(run("pe"))
            block.scalar(run("act"))
            block.vector(run("dve"))
            block.gpsimd(run("pool"))
            block.sync(run("sp"))


def build_program(debug=False, stop=None):
    cut = 0
    if stop is not None and stop.startswith("p1:"):
        cut = int(stop[3:])
        stop = "p1"
    nc = bass.Bass("TRN2", target_bir_lowering=False)
    nc.dge_precook = False
    S = Sched(nc)

    def dram(name, shape, dt, kind=None):
        if kind is None:
            return nc.dram_tensor(name, shape, dt)
        return nc.dram_tensor(name, shape, dt, kind=kind)

    x_in = dram("x", [TOK, D], F32, "ExternalInput")
    out_d = dram("out", [TOK, D], F32, "ExternalOutput")
    wsh, wbf, wall = {}, {}, {}
    used_w = {None: None, "p1": [("w_in", 0), ("wkv_b", 0)], "p2attn": [("w_in", 0), ("wkv_b", 0), ("wq_b", 0)],
              "p2out": [("w_in", 0), ("wkv_b", 0), ("wq_b", 0), ("w_out", 0)],
              "full1": [(n_, 0) for n_, _, _ in WSPEC]}[stop]
    for l in range(DEPTH):
        for name, K, N in WSPEC:
            if used_w is not None and (name, l) not in used_w:
                continue
            wsh[name, l] = dram("%s%d_sh" % (name, l), [K // 8, N], F32, "ExternalInput")
            wbf[name, l] = dram("%s%d_bf" % (name, l), [K // 8, N], BF16)
            wall[name, l] = dram("%s%d_all" % (name, l), [K, N], BF16)
    p_gpre = dram("p_gpre", [DEPTH, 128, 32], F32, "ExternalInput")
    p_gffn = dram("p_gffn", [DEPTH, 128, 32], F32, "ExternalInput")
    p_gq = dram("p_gq", [DEPTH, 128, 8], F32, "ExternalInput")
    p_gkv = dram("p_gkv", [DEPTH, 128, 4], F32, "ExternalInput")
    p_gpost = dram("p_gpost", [DEPTH, D], F32, "ExternalInput")
    p_gfpost = dram("p_gfpost", [DEPTH, D], F32, "ExternalInput")
    p_lng = dram("p_lng", [DEPTH, 1024], F32, "ExternalInput")
    p_lnb = dram("p_lnb", [DEPTH, 1024], F32, "ExternalInput")
    p_sgub = dram("p_sgub", [DEPTH, 1024], F32, "ExternalInput")
    p_sguwT = dram("p_sguwT", [DEPTH, 128, 8, 128], F32, "ExternalInput")
    c_rope = dram("c_rope", [128, 2, TOK], F32, "ExternalInput")
    c_ident = dram("c_ident", [128, 128], BF16, "ExternalInput")
    c_swap = dram("c_swap", [128, 128], BF16, "ExternalInput")
    c_rsc = dram("c_rsc", [128, 8], F32, "ExternalInput")
    c_bias = dram("c_bias", [128, 8], F32, "ExternalInput")
    c_dbd = dram("c_dbd", [128, 8, 4, 128], F32, "ExternalInput")
    c_bq = dram("c_bq", [128, 8, 512], F32, "ExternalInput")
    xs = dram("xs", [TOK, D], F32)
    xm = dram("xm", [TOK, D], F32)
    h_u = dram("h_u", [NT, 128, 8, T], BF16)
    h_cq = dram("h_cq", [NT, 128, 8, T], BF16)
    h_rq = dram("h_rq", [NT, 128, T], F32)
    h_rqT = dram("h_rqT", [NT, 64, 8, T], BF16)
    h_g = dram("h_g", [NT, 128, 8, T], BF16)
    kv_loc = dram("kv_loc", [KV_ROWS, TOK], BF16)
    yscr = dram("yscr", [TOK, D], F32)
    final_ops = []
    kv_all = [dram("kv_all%d" % l, [4 * KV_ROWS, TOK], BF16) for l in range(DEPTH)]
    kv_g = [dram("kv_g%d" % l, [8 * KV_ROWS, TOK], BF16) for l in range(DEPTH)]
    c_sel = dram("c_sel", [128, 2], F32, "ExternalInput")
    dbg = {}
    if debug:
        dbg["x1"] = dram("dbg_x1", [TOK, D], F32, "ExternalOutput")
        dbg["xm0"] = dram("dbg_xm0", [TOK, D], F32, "ExternalOutput")
        dbg["kv0"] = dram("dbg_kv0", [KV_ROWS, TOK], BF16, "ExternalOutput")
        dbg["u0"] = dram("dbg_u0", [NT, 128, 8, T], BF16, "ExternalOutput")
        dbg["mix"] = dram("dbg_mix", [NT, 128, 32, T], BF16, "ExternalOutput")

    def big_copy(dst, src, nrows, reads, step=256):
        ops_ = []
        for r in range(0, nrows, step):
            e_ = min(nrows, r + step)
            ops_.append(S.add("act", (lambda r_, e2: (lambda e: e.dma_start(out=dst[r_:e2], in_=src[r_:e2])))(r, e_),
                              reads=reads, writes=[], dma=True))
        return ops_

    import contextlib
    top = contextlib.ExitStack()
    top.__enter__()

    uid = [0]

    def sb(name, shape, dt, stack=None):
        uid[0] += 1
        return (stack or top).enter_context(nc.sbuf_tensor("%s_%d" % (name, uid[0]), shape, dt))

    ps = top.enter_context(nc.psum_tensor("ps", [128, 8, 512], F32))
    PB = [Res("bank%d" % i, excl=True) for i in range(8)]

    ident = sb("ident", [128, 128], BF16)
    swapm = sb("swapm", [128, 128], BF16)
    ones = sb("ones", [128, 128], BF16)
    eps_t = sb("eps_t", [128, 1], F32)
    ssq_p = sb("ssq", [128, 4, 8], F32)
    R_const = Res("const")
    S.add("act", lambda e: e.dma_start(out=ident[:], in_=c_ident.ap()), writes=[R_const], dma=True)
    S.add("act", lambda e: e.dma_start(out=swapm[:], in_=c_swap.ap()), writes=[R_const], dma=True)
    S.add("dve", lambda e: e.memset(ones[:], 1.0), writes=[R_const])
    S.add("dve", lambda e: e.memset(eps_t[:], EPS), writes=[R_const])

    NSLOT = 4
    ring = sb("ring", [128, NSLOT, 4096], BF16)
    RSLOT = [Res("slot%d" % i) for i in range(NSLOT)]
    ring_ctr = [0]
    R_wall = {k: Res("wall_%s%d" % k) for k in wall}

    def wload(key, src_ap, a, b):
        k = ring_ctr[0] % NSLOT
        ring_ctr[0] += 1
        view = ring[:, k, 0:a * b].rearrange("p (a b) -> p a b", a=a)
        S.add("sp", lambda e: e.dma_start(out=view, in_=src_ap), reads=[R_wall[key]], writes=[RSLOT[k]], dma=True)
        return view, RSLOT[k]

    cast_ops = {}
    for l in range(DEPTH):
        for name, K, N in WSPEC:
            if (name, l) not in wsh:
                continue
            n_el = (K // 8) * N
            f = n_el // 128
            src = wsh[name, l].ap().rearrange("r n -> (r n)").rearrange("(p f) -> p f", p=128)
            dst = wbf[name, l].ap().rearrange("r n -> (r n)").rearrange("(p f) -> p f", p=128)
            step = 8192
            lst_ = []
            for o in range(0, f, step):
                e_ = min(f, o + step)
                lst_.append(S.add("pool", (lambda s_, d_: (lambda e: e.dma_start(out=d_, in_=s_)))(src[:, o:e_], dst[:, o:e_]),
                                  writes=[], dma=True))
            cast_ops[name, l] = lst_
    prev_coll = None
    for l in range(DEPTH):
        for name, K, N in WSPEC:
            if (name, l) not in wsh:
                continue
            cop = S.add("pool", (lambda a_, b_: (lambda e: e.collective_compute(
                "AllGather", ALU.bypass, replica_groups=[list(range(NCORES))],
                ins=[a_.ap().opt()], outs=[b_.ap().opt()])))(wbf[name, l], wall[name, l]),
                reads=[], writes=[R_wall[name, l]], coll=True)
            extra = list(cast_ops[name, l]) + ([prev_coll] if prev_coll is not None else [])
            for d in extra:
                if d not in cop.deps:
                    cop.deps.add(d)
                    d.users += 1
            prev_coll = cop
    last_weight_coll = prev_coll

    def rstd_from_ss(ss_ap, out_ap, n, reads, writes, tmp_ap):
        npart = ss_ap.shape[0]
        S.add("act", lambda e: e.activation(out=tmp_ap, in_=ss_ap, func=AF.Sqrt, bias=eps_t[0:npart, :], scale=1.0 / n),
              reads=reads + [R_const], writes=[writes[0]])
        S.add("dve", lambda e: e.reciprocal(out=out_ap, in_=tmp_ap), reads=[writes[0]], writes=writes)

    stopped = False
    for l in range(DEPTH):
        if stopped:
            break
        x_src = x_in if l == 0 else xs
        x_dst = xs if l == 0 else out_d
        R_xsrc = Res("xsrc")
        R_kvloc = Res("kvloc")
        R_hand = [Res("hand%d" % t) for t in range(NT)]
        R_xm = Res("xm")
        R_xdst = Res("xdst")
        S.fence()

        for t in range(NT):
            S.fence()
            st = contextlib.ExitStack()
            st.__enter__()
            xb = sb("xb", [128, 1, D], F32, st)
            hb = sb("hb", [128, 1, D], BF16, st)
            hT = sb("hT", [128, 32, T], BF16, st)
            gpre = sb("gpre", [128, 32], F32, st)
            gq = sb("gq", [128, 8], F32, st)
            gkv = sb("gkv", [128, 4], F32, st)
            stat = sb("stat", [128, 16], F32, st)
            rope = sb("rope", [128, 2, T], F32, st)
            uT = sb("uT", [128, 8, T], BF16, st)
            vblk = sb("vblk", [128, 4, 1024], F32, st)
            vn = sb("vn", [128, 4, 1024], BF16, st)
            lng = sb("lng", [128, 1024], F32, st)
            lnb = sb("lnb", [128, 1024], F32, st)
            sgb = sb("sgb", [128, 1024], F32, st)
            wsT32 = sb("wsT32", [128, 8, 128], F32, st)
            wsT = sb("wsT", [128, 8, 128], BF16, st)
            cqg = sb("cqg", [128, 8, T], BF16, st)
            ckvg = sb("ckvg", [128, 4, T], BF16, st)
            sq = sb("sq", [128, 2, T], BF16, st)
            t32 = sb("t32", [128, 4, T], F32, st)
            rbc = sb("rbc", [128, 2, T], F32, st)
            rkvcol = sb("rkvcol", [128, 8], F32, st)
            obf = sb("obf", [128, 2, 1024], BF16, st)
            gsil = sb("gsil", [128, 8, T], BF16, st)
            rqT = sb("rqT", [128, 4, T], BF16, st)
            rsc = sb("rsc", [128, 8], F32, st)

            R_xb = [Res("xb0"), Res("xb1")]
            R_hb = [Res("hb0"), Res("hb1")]
            R_hT = Res("hT")
            R_par = Res("par")
            R_stat = Res("stat")
            R_uT, R_cqg, R_ckvg = Res("uT"), Res("cqg"), Res("ckvg")
            R_vblk = [Res("vblk%d" % i) for i in range(4)]
            R_vn = [Res("vn%d" % i) for i in range(4)]
            R_sq = [Res("sq0"), Res("sq1")]
            R_t32 = [Res("t32_%d" % i) for i in range(4)]
            R_rbc = [Res("rbc0"), Res("rbc1")]
            R_rkv = Res("rkvcol")
            R_obf = [Res("obf0"), Res("obf1")]
            R_gsil, R_rqT = Res("gsil"), Res("rqT")
            obf_ctr = [0]

            def ld(dst, src, res):
                S.add("act", lambda e: e.dma_start(out=dst, in_=src), writes=[res], dma=True)

            for _once in (0,):
                ld(gpre[:], p_gpre[l], R_par)
                ld(gq[:], p_gq[l], R_par)
                ld(gkv[:], p_gkv[l], R_par)
                ld(rsc[:], c_rsc.ap(), R_par)
                ld(rope[:], c_rope[:, :, t * T:(t + 1) * T], R_par)
                ld(lng[:], p_lng[l].partition_broadcast(128), R_par)
                ld(lnb[:], p_lnb[l].partition_broadcast(128), R_par)
                ld(sgb[:], p_sgub[l].partition_broadcast(128), R_par)
                ld(wsT32[:], p_sguwT[l], R_par)
                S.add("dve", lambda e: e.memset(wsT32[64:128, :, 0:64], 0.0), reads=[R_par], writes=[R_par])
                S.add("dve", lambda e: e.tensor_copy(out=wsT[:], in_=wsT32[:]), reads=[R_par], writes=[R_par])

                for blk in range(4):
                    b2 = 0
                    r0 = t * T + blk * 128
                    S.add("act", (lambda b2_, r0_: (lambda e: e.dma_start(out=xb[:, b2_, :], in_=x_src[r0_:r0_ + 128, :])))(b2, r0),
                          reads=[R_xsrc], writes=[R_xb[b2]], dma=True)
                    S.add("act", (lambda b2_, blk_: (lambda e: e.activation(out=hb[:, b2_, :], in_=xb[:, b2_, :], func=AF.Square,
                                                                           accum_out=stat[:, blk_:blk_ + 1])))(b2, blk),
                          reads=[R_xb[b2]], writes=[R_hb[b2], R_stat])
                    rstd_from_ss(stat[:, blk:blk + 1], stat[:, 8 + blk:9 + blk], D, [R_stat], [R_stat], stat[:, 4 + blk:5 + blk])
                    S.add("act", (lambda b2_, blk_: (lambda e: e.activation(out=hb[:, b2_, :], in_=xb[:, b2_, :], func=AF.Copy,
                                                                           scale=stat[:, 8 + blk_:9 + blk_])))(b2, blk),
                          reads=[R_xb[b2], R_stat], writes=[R_hb[b2]])
                    for cg in range(8):
                        bank = cg % 2
                        pv = ps[:, bank, :].bitcast(BF16)
                        for c4 in range(4):
                            c = cg * 4 + c4
                            S.add("pe", (lambda pv_, c4_, b2_, c_: (lambda e: e.transpose(
                                out=pv_[:, c4_ * 128:(c4_ + 1) * 128], in_=hb[:, b2_, c_ * 128:(c_ + 1) * 128], identity=ident[:])))(pv, c4, b2, c),
                                reads=[R_hb[b2], R_const], writes=[PB[bank]])
                        for c4 in range(4):
                            c = cg * 4 + c4
                            S.add("dve", (lambda pv_, c4_, c_, blk_: (lambda e: e.tensor_scalar(
                                out=hT[:, c_, blk_ * 128:(blk_ + 1) * 128], in0=pv_[:, c4_ * 128:(c4_ + 1) * 128],
                                scalar1=gpre[:, c_:c_ + 1], scalar2=None, op0=ALU.mult)))(pv, c4, c, blk),
                                reads=[PB[bank], R_par], writes=[R_hT])

                if cut == 1:
                    break
                def proj_fm(col0, ncols, evac):
                    nch = (ncols + 127) // 128
                    for g0 in range(0, nch, 4):
                        gc = min(4, nch - g0)
                        cw = min(512, ncols - g0 * 128)
                        for kt in range(4):
                            wv, wr = wload(("w_in", l), wall["w_in", l].ap().rearrange("(kc p) n -> p kc n", p=128)[:, kt * 8:(kt + 1) * 8, col0 + g0 * 128: col0 + g0 * 128 + cw], 8, cw)
                            for kc in range(8):
                                k = kt * 8 + kc
                                for j in range(gc):
                                    m = min(128, cw - j * 128)
                                    S.add("pe", (lambda wv_, kc_, j_, m_, k_: (lambda e: e.matmul(
                                        ps[0:m_, 4 + j_, :], lhsT=wv_[:, kc_, j_ * 128:j_ * 128 + m_], rhs=hT[:, k_, :],
                                        start=(k_ == 0), stop=(k_ == 31))))(wv, kc, j, m, k),
                                        reads=[wr, R_hT], writes=[PB[4 + j]])
                        for j in range(gc):
                            m = min(128, cw - j * 128)
                            evac(g0 + j, ps[0:m, 4 + j, :], PB[4 + j], m)

                def proj_tm(col0, ncols, evac):
                    for g in range(ncols // 512):
                        for kt in range(4):
                            wv, wr = wload(("w_in", l), wall["w_in", l].ap().rearrange("(kc p) n -> p kc n", p=128)[:, kt * 8:(kt + 1) * 8, col0 + g * 512: col0 + (g + 1) * 512], 8, 512)
                            for kc in range(8):
                                k = kt * 8 + kc
                                for blk in range(4):
                                    S.add("pe", (lambda wv_, kc_, blk_, k_: (lambda e: e.matmul(
                                        ps[:, 4 + blk_, :], lhsT=hT[:, k_, blk_ * 128:(blk_ + 1) * 128], rhs=wv_[:, kc_, :],
                                        start=(k_ == 0), stop=(k_ == 31))))(wv, kc, blk, k),
                                        reads=[wr, R_hT], writes=[PB[4 + blk]])
                        for blk in range(4):
                            evac(blk, g, ps[:, 4 + blk, :], PB[4 + blk])

                def ev_u(j, pa, br, m):
                    S.add("act", lambda e: e.activation(out=uT[:, j, :], in_=pa, func=AF.Copy), reads=[br], writes=[R_uT])
                proj_fm(O_U, 1024, ev_u)
                if cut == 2:
                    break

                def ev_v(blk, g, pa, br):
                    S.add("act", lambda e: e.activation(out=vblk[:, blk, g * 512:(g + 1) * 512], in_=pa, func=AF.Copy),
                          reads=[br], writes=[R_vblk[blk]])
                proj_tm(O_V, 1024, ev_v)
                lst = sb("lst", [128, 4, 8, 8], F32, st)
                lagg = sb("lagg", [128, 4, 8, 4], F32, st)
                R_lst = Res("lst")
                for blk in range(4):
                    for g in range(8):
                        S.add("dve", (lambda blk_, g_: (lambda e: e.bn_stats(out=lst[:, blk_, g_, 0:6], in_=vblk[:, blk_, g_ * 128:(g_ + 1) * 128])))(blk, g),
                              reads=[R_vblk[blk]], writes=[R_lst])
                        S.add("dve", (lambda blk_, g_: (lambda e: e.bn_aggr(out=lagg[:, blk_, g_, 0:2], in_=lst[:, blk_, g_, 0:6])))(blk, g),
                              reads=[R_lst], writes=[R_lst])
                        S.add("act", (lambda blk_, g_: (lambda e: e.activation(out=lagg[:, blk_, g_, 2:3], in_=lagg[:, blk_, g_, 1:2],
                                                                              func=AF.Sqrt, bias=eps_t[:, :], scale=1.0)))(blk, g),
                              reads=[R_lst, R_const], writes=[R_lst])
                        S.add("dve", (lambda blk_, g_: (lambda e: e.reciprocal(out=lagg[:, blk_, g_, 3:4], in_=lagg[:, blk_, g_, 2:3])))(blk, g),
                              reads=[R_lst], writes=[R_lst])
                        S.add("dve", (lambda blk_, g_: (lambda e: e.tensor_scalar(
                            out=vblk[:, blk_, g_ * 128:(g_ + 1) * 128], in0=vblk[:, blk_, g_ * 128:(g_ + 1) * 128],
                            scalar1=lagg[:, blk_, g_, 0:1], scalar2=lagg[:, blk_, g_, 3:4], op0=ALU.subtract, op1=ALU.mult)))(blk, g),
                            reads=[R_lst, R_vblk[blk]], writes=[R_vblk[blk]])
                    S.add("dve", (lambda blk_: (lambda e: e.tensor_tensor(out=vblk[:, blk_, :], in0=vblk[:, blk_, :], in1=lng[:], op=ALU.mult)))(blk),
                          reads=[R_vblk[blk], R_par], writes=[R_vblk[blk]])
                    S.add("dve", (lambda blk_: (lambda e: e.tensor_tensor(out=vn[:, blk_, :], in0=vblk[:, blk_, :], in1=lnb[:], op=ALU.add)))(blk),
                          reads=[R_vblk[blk], R_par], writes=[R_vn[blk]])
                for g in range(8):
                    bank = g % 2
                    for blk in range(4):
                        S.add("pe", (lambda g_, blk_, bank_: (lambda e: e.matmul(
                            ps[:, bank_, blk_ * 128:(blk_ + 1) * 128], lhsT=vn[:, blk_, g_ * 128:(g_ + 1) * 128], rhs=wsT[:, g_, :],
                            start=True, stop=True)))(g, blk, bank),
                            reads=[R_vn[blk], R_par], writes=[PB[bank]])
                    for blk in range(4):
                        S.add("dve", (lambda g_, blk_, bank_: (lambda e: e.tensor_tensor(
                            out=t32[:, 0, blk_ * 128:(blk_ + 1) * 128], in0=ps[:, bank_, blk_ * 128:(blk_ + 1) * 128],
                            in1=sgb[:, g_ * 128:(g_ + 1) * 128], op=ALU.add)))(g, blk, bank),
                            reads=[PB[bank], R_par], writes=[R_t32[0]])
                    S.add("dve", (lambda g_: (lambda e: e.tensor_tensor(out=uT[:, g_, :], in0=uT[:, g_, :], in1=t32[:, 0, :], op=ALU.mult)))(g),
                          reads=[R_t32[0], R_uT], writes=[R_uT])
                S.add("act", lambda e: e.dma_start(out=h_u[t], in_=uT[:]), reads=[R_uT], writes=[R_hand[t]], dma=True)
                if cut == 3:
                    break

                def ev_cq(j, pa, br, m):
                    S.add("dve", lambda e: e.tensor_scalar(out=cqg[:, j, :], in0=pa, scalar1=gq[:, j:j + 1], scalar2=None, op0=ALU.mult),
                          reads=[br, R_par], writes=[R_cqg])
                cq_sq_ops = []

                def ev_cq2(j, pa, br, m):
                    s2 = j % 2
                    S.add("act", lambda e: e.activation(out=sq[:, s2, :], in_=pa, func=AF.Square), reads=[br], writes=[R_sq[s2]])
                    S.add("pe", lambda e: e.matmul(ps[:, 2, :], lhsT=ones[:], rhs=sq[:, s2, :], start=(j == 0), stop=(j == 7)),
                          reads=[R_sq[s2], R_const], writes=[PB[2]])
                    ev_cq(j, pa, br, m)
                proj_fm(O_CQ, 1024, ev_cq2)
                rstd_from_ss(ps[:, 2, :], rbc[:, 0, :], 1024, [PB[2]], [R_rbc[0]], t32[:, 1, :])
                S.add("act", lambda e: e.dma_start(out=h_cq[t], in_=cqg[:]), reads=[R_cqg], writes=[R_hand[t]], dma=True)
                S.add("act", lambda e: e.dma_start(out=h_rq[t], in_=rbc[:, 0, :]), reads=[R_rbc[0]], writes=[R_hand[t]], dma=True)
                if cut == 4:
                    break

                def ev_ckv(j, pa, br, m):
                    s2 = j % 2
                    S.add("act", lambda e: e.activation(out=sq[:, s2, :], in_=pa, func=AF.Square), reads=[br], writes=[R_sq[s2]])
                    S.add("pe", lambda e: e.matmul(ps[:, 2, :], lhsT=ones[:], rhs=sq[:, s2, :], start=(j == 0), stop=(j == 3)),
                          reads=[R_sq[s2], R_const], writes=[PB[2]])
                    for blk in range(4):
                        S.add("pe", (lambda blk_: (lambda e: e.matmul(ps[:, 3, blk_ * 4 + j: blk_ * 4 + j + 1], lhsT=sq[:, s2, blk_ * 128:(blk_ + 1) * 128],
                                                                     rhs=ones[:, 0:1], start=True, stop=True)))(blk),
                              reads=[R_sq[s2], R_const], writes=[PB[3]])
                    S.add("dve", lambda e: e.tensor_scalar(out=ckvg[:, j, :], in0=pa, scalar1=gkv[:, j:j + 1], scalar2=None, op0=ALU.mult),
                          reads=[br, R_par], writes=[R_ckvg])
                proj_fm(O_CKV, 512, ev_ckv)
                rstd_from_ss(ps[:, 2, :], rbc[:, 1, :], 512, [PB[2]], [R_rbc[1]], t32[:, 1, :])
                S.add("dve", lambda e: e.tensor_reduce(out=rkvcol[:, 0:4], in_=ps[:, 3, 0:16].rearrange("p (b j) -> p b j", j=4), axis=AX.X, op=ALU.add),
                      reads=[PB[3]], writes=[R_rkv])
                S.add("act", lambda e: e.activation(out=t32[:, 1, 0:4], in_=rkvcol[:, 0:4], func=AF.Sqrt, bias=eps_t[:, :], scale=1.0 / 512),
                      reads=[R_rkv, R_const], writes=[R_t32[1]])
                S.add("dve", lambda e: e.reciprocal(out=rkvcol[:, 4:8], in_=t32[:, 1, 0:4]), reads=[R_t32[1]], writes=[R_rkv])
                if cut == 5:
                    break

                def rope_apply(a32, res_a, P, out_ap, res_out, scale, bank):
                    S.add("act", lambda e: e.activation(out=sq[0:P, 0, :], in_=a32, func=AF.Copy), reads=[res_a], writes=[R_sq[0]])
                    S.add("pe", lambda e: e.matmul(ps[0:P, bank, :], lhsT=swapm[0:P, 0:P], rhs=sq[0:P, 0, :], start=True, stop=True),
                          reads=[R_sq[0], R_const], writes=[PB[bank]])
                    S.add("dve", lambda e: e.tensor_tensor(out=t32[0:P, 2, :], in0=ps[0:P, bank, :], in1=rope[0:P, 1, :], op=ALU.mult),
                          reads=[PB[bank], R_par], writes=[R_t32[2]])
                    S.add("dve", lambda e: e.tensor_tensor(out=t32[0:P, 3, :], in0=a32, in1=rope[0:P, 0, :], op=ALU.mult),
                          reads=[res_a, R_par], writes=[R_t32[3]])
                    if scale is None:
                        S.add("dve", lambda e: e.tensor_tensor(out=out_ap, in0=t32[0:P, 2, :], in1=t32[0:P, 3, :], op=ALU.add),
                              reads=[R_t32[2], R_t32[3]], writes=[res_out])
                    else:
                        S.add("dve", lambda e: e.tensor_tensor(out=t32[0:P, 2, :], in0=t32[0:P, 2, :], in1=t32[0:P, 3, :], op=ALU.add),
                              reads=[R_t32[2], R_t32[3]], writes=[R_t32[2]])
                        S.add("act", lambda e: e.activation(out=out_ap, in_=t32[0:P, 2, :], func=AF.Copy, scale=scale),
                              reads=[R_t32[2], R_par], writes=[res_out])

                def stage(nrows_view):
                    k = obf_ctr[0] % 2
                    obf_ctr[0] += 1
                    return k

                def ev_kr(j, pa, br, m):
                    S.add("act", lambda e: e.activation(out=t32[0:64, 1, :], in_=pa, func=AF.Copy), reads=[br], writes=[R_t32[1]])
                    k = stage(0)
                    rope_apply(t32[0:64, 1, :], R_t32[1], 64, obf[0:64, k, 0:T], R_obf[k], None, 3)
                    S.add("act", lambda e: e.dma_start(out=kv_loc[2048:2048 + 64, t * T:(t + 1) * T], in_=obf[0:64, k, 0:T]),
                          reads=[R_obf[k]], writes=[R_kvloc], dma=True)
                proj_fm(O_KR, 64, ev_kr)
                if cut == 6:
                    break

                def ev_rq(j, pa, br, m):
                    S.add("act", lambda e: e.activation(out=t32[:, 1, :], in_=pa, func=AF.Copy), reads=[br], writes=[R_t32[1]])
                    rope_apply(t32[:, 1, :], R_t32[1], 128, rqT[:, j, :], R_rqT, rsc[:, j:j + 1], 3)
                proj_fm(O_RQ, 512, ev_rq)
                for hh in range(8):
                    S.add("act", (lambda hh_: (lambda e: e.dma_start(out=h_rqT[t][:, hh_, :], in_=rqT[(hh_ % 2) * 64:(hh_ % 2) * 64 + 64, hh_ // 2, :])))(hh),
                          reads=[R_rqT], writes=[R_hand[t]], dma=True)

                def ev_rk(j, pa, br, m):
                    S.add("act", lambda e: e.activation(out=t32[:, 1, :], in_=pa, func=AF.Copy), reads=[br], writes=[R_t32[1]])
                    k = stage(0)
                    rope_apply(t32[:, 1, :], R_t32[1], 128, obf[:, k, 0:T], R_obf[k], rsc[:, 4 + j:5 + j], 3)
                    S.add("act", lambda e: e.dma_start(out=kv_loc[2112 + j * 128:2112 + (j + 1) * 128, t * T:(t + 1) * T], in_=obf[:, k, 0:T]),
                          reads=[R_obf[k]], writes=[R_kvloc], dma=True)
                proj_fm(O_RK, 512, ev_rk)
                if cut == 8:
                    break

                kvV = kv_loc.ap()[KT_ROWS:KV_ROWS, :].rearrange("(p h) (b d) -> p h b d", h=24, d=128)

                def ev_rv(blk, g, pa, br):
                    k = stage(0)
                    S.add("act", lambda e: e.activation(out=obf[:, k, 0:512], in_=pa, func=AF.Copy), reads=[br], writes=[R_obf[k]])
                    S.add("act", lambda e: e.dma_start(out=kvV[:, 16 + g * 4:16 + g * 4 + 4, t * 4 + blk, :],
                                                       in_=obf[:, k, 0:512].rearrange("p (h d) -> p h d", d=128)),
                          reads=[R_obf[k]], writes=[R_kvloc], dma=True)
                proj_tm(O_RV, 1024, ev_rv)
                if cut == 9:
                    break

                def ev_rg(j, pa, br, m):
                    S.add("act", lambda e: e.activation(out=gsil[:, j, :], in_=pa, func=AF.Silu), reads=[br], writes=[R_gsil])
                proj_fm(O_RG, 1024, ev_rg)
                S.add("act", lambda e: e.dma_start(out=h_g[t], in_=gsil[:]), reads=[R_gsil], writes=[R_hand[t]], dma=True)
                if cut == 10:
                    break

                wkv = wall["wkv_b", l].ap().rearrange("(kc p) n -> p kc n", p=128)
                for hg in range(4):
                    wv, wr = wload(("wkv_b", l), wkv[:, :, hg * 1024:(hg + 1) * 1024], 4, 1024)
                    for h4 in range(4):
                        h = hg * 4 + h4
                        bank = 4 + (h % 2)
                        for kc in range(4):
                            S.add("pe", (lambda wv_, kc_, h4_, bank_: (lambda e: e.matmul(
                                ps[:, bank_, :], lhsT=wv_[:, kc_, h4_ * 256:h4_ * 256 + 128], rhs=ckvg[:, kc_, :],
                                start=(kc_ == 0), stop=(kc_ == 3))))(wv, kc, h4, bank),
                                reads=[wr, R_ckvg], writes=[PB[bank]])
                        k = stage(0)
                        S.add("dve", (lambda bank_, k_: (lambda e: e.tensor_tensor(out=obf[:, k_, 0:T], in0=ps[:, bank_, :], in1=rbc[:, 1, :], op=ALU.mult)))(bank, k),
                              reads=[PB[bank], R_rbc[1]], writes=[R_obf[k]])
                        S.add("act", (lambda h_, k_: (lambda e: e.dma_start(out=kv_loc[h_ * 128:(h_ + 1) * 128, t * T:(t + 1) * T], in_=obf[:, k_, 0:T])))(h, k),
                              reads=[R_obf[k]], writes=[R_kvloc], dma=True)
                    for blk in range(4):
                        bank = 6 + (blk % 2)
                        for kc in range(4):
                            S.add("pe", (lambda wv_, kc_, blk_, bank_: (lambda e: e.matmul(
                                ps[:, bank_, :].rearrange("p (h d) -> p h d", d=128), lhsT=ckvg[:, kc_, blk_ * 128:(blk_ + 1) * 128],
                                rhs=wv_[:, kc_, :].rearrange("p (h two d) -> p h two d", two=2, d=128)[:, :, 1, :],
                                start=(kc_ == 0), stop=(kc_ == 3))))(wv, kc, blk, bank),
                                reads=[wr, R_ckvg], writes=[PB[bank]])
                        k = stage(0)
                        S.add("dve", (lambda bank_, k_, blk_: (lambda e: e.tensor_scalar(out=obf[:, k_, 0:512], in0=ps[:, bank_, :],
                                                                                         scalar1=rkvcol[:, 4 + blk_:5 + blk_], scalar2=None, op0=ALU.mult)))(bank, k, blk),
                              reads=[PB[bank], R_rkv], writes=[R_obf[k]])
                        S.add("act", (lambda hg_, k_, blk_: (lambda e: e.dma_start(out=kvV[:, hg_ * 4:hg_ * 4 + 4, t * 4 + blk_, :],
                                                                                   in_=obf[:, k_, 0:512].rearrange("p (h d) -> p h d", d=128))))(hg, k, blk),
                              reads=[R_obf[k]], writes=[R_kvloc], dma=True)
            S.fence()
            st.close()

        if stop == "p1":
            final_ops.extend(big_copy(dbg["kv0"], kv_loc, KV_ROWS, [R_kvloc]))
            for t_ in range(NT):
                final_ops.extend(big_copy(dbg["u0"][t_], h_u[t_], 128, [R_hand[t_]]))
            stopped = True
            break
        if debug and l == 0:
            final_ops.extend(big_copy(dbg["kv0"], kv_loc, KV_ROWS, [R_kvloc]))
            for t_ in range(NT):
                final_ops.extend(big_copy(dbg["u0"][t_], h_u[t_], 128, [R_hand[t_]]))
        S.fence()
        R_kvall = Res("kvall")
        R_kvg = Res("kvg")
        S.add("pool", (lambda l_: (lambda e: e.collective_compute(
            "AllGather", ALU.bypass, replica_groups=[list(range(NCORES))],
            ins=[kv_loc.ap().opt()], outs=[kv_g[l_].ap().opt()])))(l),
            reads=[R_kvloc], writes=[R_kvg], coll=True)
        S.fence()
        ss = contextlib.ExitStack()
        ss.__enter__()
        selA = sb("selA", [128, 2, 8, 1024], BF16, ss)
        selB = sb("selB", [128, 2, 8, 1024], BF16, ss)
        selm = sb("selm", [128, 2], F32, ss)
        R_sA, R_sB, R_selm = [Res("sA0"), Res("sA1")], [Res("sB0"), Res("sB1")], Res("selm")
        S.add("act", lambda e: e.dma_start(out=selm[:], in_=c_sel.ap()), writes=[R_selm], dma=True)

        def sel_chunk(gA, gB, dst, a0, a1, par):
            n = a1 - a0
            va = selA[0:64, par, 0:n, :]
            vb = selB[0:64, par, 0:n, :]
            S.add("act", lambda e: e.dma_start(out=va, in_=gA[:, a0:a1, :]), reads=[R_kvg], writes=[R_sA[par]], dma=True)
            S.add("act", lambda e: e.dma_start(out=vb, in_=gB[:, a0:a1, :]), reads=[R_kvg], writes=[R_sB[par]], dma=True)
            S.add("dve", lambda e: e.tensor_scalar(out=va, in0=va, scalar1=selm[0:64, 0:1], scalar2=None, op0=ALU.mult),
                  reads=[R_sA[par], R_selm], writes=[R_sA[par]])
            S.add("dve", lambda e: e.tensor_scalar(out=vb, in0=vb, scalar1=selm[0:64, 1:2], scalar2=None, op0=ALU.mult),
                  reads=[R_sB[par], R_selm], writes=[R_sB[par]])
            S.add("dve", lambda e: e.tensor_tensor(out=va, in0=va, in1=vb, op=ALU.add), reads=[R_sA[par], R_sB[par]], writes=[R_sA[par]])
            S.add("act", lambda e: e.dma_start(out=dst[:, a0:a1, :], in_=va), reads=[R_sA[par]], writes=[R_kvall], dma=True)

        sel_it = 0
        for c in range(4):
            gA = kv_g[l].ap()[c * KV_ROWS:(c + 1) * KV_ROWS, :].rearrange("(p a) n -> p a n", p=64)
            gB = kv_g[l].ap()[(4 + c) * KV_ROWS:(5 + c) * KV_ROWS, :].rearrange("(p a) n -> p a n", p=64)
            dstv = kv_all[l].ap()[c * KV_ROWS:(c + 1) * KV_ROWS, :].rearrange("(p a) n -> p a n", p=64)
            for a0 in range(0, KV_ROWS // 64, 8):
                sel_chunk(gA, gB, dstv, a0, min(KV_ROWS // 64, a0 + 8), sel_it % 2)
                sel_it += 1
        S.fence()
        ss.close()
        kva = kv_all[l].ap()

        for t in range(NT):
            S.fence()
            st = contextlib.ExitStack()
            st.__enter__()
            mixT = sb("mixT", [128, 32, T], BF16, st)
            R_mix = Res("mixT")
            sa = contextlib.ExitStack()
            sa.__enter__()
            cqg2 = sb("cqg2", [128, 8, T], BF16, sa)
            rqbc = sb("rqbc", [128, T], F32, sa)
            rope2 = sb("rope2", [128, 2, T], F32, sa)
            qn = sb("qn", [128, 2, T], BF16, sa)
            qr = sb("qr", [128, 2, T], BF16, sa)
            KTh = sb("KTh", [128, 2, 4, 1024], BF16, sa)
            Vh = sb("Vh", [128, 2, 4, 8, 128], BF16, sa)
            krT = sb("krT", [128, 4, 1024], BF16, sa)
            pT = sb("pT", [128, 3, T], BF16, sa)
            u32 = sb("u32", [128, 4, T], F32, sa)
            biasm = sb("biasm", [128, 8], F32, sa)
            bq = sb("bq", [128, 2, 512], F32, sa)
            dbd = sb("dbd", [128, 2, 4, 128], F32, sa)
            rqT2 = sb("rqT2", [128, 8, T], BF16, sa)
            gs = sb("gs", [128, 8, T], BF16, sa)
            sq2 = sb("sq2", [128, T], BF16, sa)
            R_p2 = Res("p2par")
            R_q = [Res("q0"), Res("q1")]
            R_KT = [Res("KT0"), Res("KT1")]
            R_V = [Res("V0"), Res("V1")]
            R_pT = [Res("pT%d" % i) for i in range(3)]
            R_u32 = [Res("u32_%d" % i) for i in range(4)]
            R_bq = [Res("bq0"), Res("bq1")]
            R_sq2 = Res("sq2")

            def ld2(dst, src, res, reads=()):
                S.add("act", lambda e: e.dma_start(out=dst, in_=src), reads=list(reads), writes=[res], dma=True)

            ld2(cqg2[:], h_cq[t], R_p2, [R_hand[t]])
            ld2(rqbc[:], h_rq[t], R_p2, [R_hand[t]])
            ld2(rope2[:], c_rope[:, :, t * T:(t + 1) * T], R_p2)
            ld2(biasm[:], c_bias.ap(), R_p2)
            ld2(rqT2[0:64, :, :], h_rqT[t], R_p2, [R_hand[t]])
            ld2(gs[:], h_g[t], R_p2, [R_hand[t]])
            ld2(mixT[:, 0:8, :], h_u[t], R_mix, [R_hand[t]])
            for c in range(4):
                ld2(krT[0:64, c, :], kva[c * KV_ROWS + 2048:c * KV_ROWS + 2112, :], R_p2, [R_kvall])

            def kvV(c):
                return kva[c * KV_ROWS + KT_ROWS:(c + 1) * KV_ROWS, :].rearrange("(p h) (b d) -> p h b d", h=24, d=128)

            nG = 4 * t + 4
            wq = wall["wq_b", l].ap().rearrange("(kc p) n -> p kc n", p=128)
            sctr = [0]
            pctr = [0]

            def rope2_apply(a32, res_a, P, out_ap, res_out, bank):
                S.add("act", lambda e: e.activation(out=sq2[0:P, :], in_=a32, func=AF.Copy), reads=[res_a], writes=[R_sq2])
                S.add("pe", lambda e: e.matmul(ps[0:P, bank, :], lhsT=swapm[0:P, 0:P], rhs=sq2[0:P, :], start=True, stop=True),
                      reads=[R_sq2, R_const], writes=[PB[bank]])
                S.add("dve", lambda e: e.tensor_tensor(out=u32[0:P, 2, :], in0=ps[0:P, bank, :], in1=rope2[0:P, 1, :], op=ALU.mult),
                      reads=[PB[bank], R_p2], writes=[R_u32[2]])
                S.add("dve", lambda e: e.tensor_tensor(out=u32[0:P, 3, :], in0=a32, in1=rope2[0:P, 0, :], op=ALU.mult),
                      reads=[res_a, R_p2], writes=[R_u32[3]])
                S.add("dve", lambda e: e.tensor_tensor(out=out_ap, in0=u32[0:P, 2, :], in1=u32[0:P, 3, :], op=ALU.add),
                      reads=[R_u32[2], R_u32[3]], writes=[res_out])

            for hp in range(8):
                wv, wr = wload(("wq_b", l), wq[:, :, hp * 384:(hp + 1) * 384], 8, 384)
                for h2 in range(2):
                    h = hp * 2 + h2
                    hb2 = h % 2
                    for kc in range(8):
                        S.add("pe", (lambda kc_: (lambda e: e.matmul(ps[:, 6, :], lhsT=wv[:, kc_, h2 * 192:h2 * 192 + 128], rhs=cqg2[:, kc_, :],
                                                                    start=(kc_ == 0), stop=(kc_ == 7))))(kc), reads=[wr, R_p2], writes=[PB[6]])
                    S.add("dve", lambda e: e.tensor_tensor(out=qn[:, hb2, :], in0=ps[:, 6, :], in1=rqbc[:], op=ALU.mult),
                          reads=[PB[6], R_p2], writes=[R_q[hb2]])
                    for kc in range(8):
                        S.add("pe", (lambda kc_: (lambda e: e.matmul(ps[0:64, 7, :], lhsT=wv[:, kc_, h2 * 192 + 128:h2 * 192 + 192], rhs=cqg2[:, kc_, :],
                                                                    start=(kc_ == 0), stop=(kc_ == 7))))(kc), reads=[wr, R_p2], writes=[PB[7]])
                    S.add("dve", lambda e: e.tensor_tensor(out=u32[0:64, 1, :], in0=ps[0:64, 7, :], in1=rqbc[0:64, :], op=ALU.mult),
                          reads=[PB[7], R_p2], writes=[R_u32[1]])
                    rope2_apply(u32[0:64, 1, :], R_u32[1], 64, qr[0:64, hb2, :], R_q[hb2], 7)
                    for c in range(4):
                        ld2(KTh[:, hb2, c, :], kva[c * KV_ROWS + 128 * h:c * KV_ROWS + 128 * (h + 1), :], R_KT[hb2], [R_kvall])
                        ld2(Vh[:, hb2, c, :, :], kvV(c)[:, h, :, :], R_V[hb2], [R_kvall])
                    ob = 0 if hb2 == 0 else 4
                    first = True
                    for G in range(nG):
                        col0 = max(0, (G - 4 * t) * 128)
                        for c in range(4):
                            sbk = 2 + (sctr[0] % 2)
                            sctr[0] += 1
                            pk = pctr[0] % 3
                            pctr[0] += 1
                            S.add("pe", (lambda G_, c_, sbk_, col0_: (lambda e: e.matmul(
                                ps[:, sbk_, col0_:T], lhsT=KTh[:, hb2, c_, G_ * 128:(G_ + 1) * 128], rhs=qn[:, hb2, col0_:T],
                                start=True, stop=False)))(G, c, sbk, col0), reads=[R_KT[hb2], R_q[hb2]], writes=[PB[sbk]])
                            S.add("pe", (lambda G_, c_, sbk_, col0_: (lambda e: e.matmul(
                                ps[:, sbk_, col0_:T], lhsT=krT[0:64, c_, G_ * 128:(G_ + 1) * 128], rhs=qr[0:64, hb2, col0_:T],
                                start=False, stop=True)))(G, c, sbk, col0), reads=[R_p2, R_q[hb2]], writes=[PB[sbk]])
                            if G >= 4 * t:
                                S.add("act", (lambda c_, sbk_, pk_, col0_: (lambda e: e.activation(
                                    out=pT[:, pk_, col0_:col0_ + 64], in_=ps[:, sbk_, col0_:col0_ + 64], func=AF.Exp,
                                    bias=biasm[:, c_:c_ + 1], scale=ATT_SCALE)))(c, sbk, pk, col0), reads=[PB[sbk], R_p2], writes=[R_pT[pk]])
                                S.add("act", (lambda c_, sbk_, pk_, col0_: (lambda e: e.activation(
                                    out=pT[:, pk_, col0_ + 64:col0_ + 128], in_=ps[:, sbk_, col0_ + 64:col0_ + 128], func=AF.Exp,
                                    bias=biasm[:, 4 + c_:5 + c_], scale=ATT_SCALE)))(c, sbk, pk, col0), reads=[PB[sbk], R_p2], writes=[R_pT[pk]])
                                if col0 + 128 < T:
                                    S.add("act", (lambda sbk_, pk_, col0_: (lambda e: e.activation(
                                        out=pT[:, pk_, col0_ + 128:T], in_=ps[:, sbk_, col0_ + 128:T], func=AF.Exp, scale=ATT_SCALE)))(sbk, pk, col0),
                                        reads=[PB[sbk]], writes=[R_pT[pk]])
                            else:
                                S.add("act", (lambda sbk_, pk_: (lambda e: e.activation(
                                    out=pT[:, pk_, :], in_=ps[:, sbk_, :], func=AF.Exp, scale=ATT_SCALE)))(sbk, pk),
                                    reads=[PB[sbk]], writes=[R_pT[pk]])
                            last = (G == nG - 1 and c == 3)
                            S.add("pe", (lambda G_, c_, pk_, col0_, first_, last_: (lambda e: e.matmul(
                                ps[:, ob, col0_:T], lhsT=Vh[:, hb2, c_, G_, :], rhs=pT[:, pk_, col0_:T], start=first_, stop=last_)))(G, c, pk, col0, first, last),
                                reads=[R_V[hb2], R_pT[pk]], writes=[PB[ob]])
                            S.add("pe", (lambda pk_, col0_, first_, last_: (lambda e: e.matmul(
                                ps[:, ob + 1, col0_:T], lhsT=ones[:], rhs=pT[:, pk_, col0_:T], start=first_, stop=last_)))(pk, col0, first, last),
                                reads=[R_const, R_pT[pk]], writes=[PB[ob + 1]])
                            first = False
                    S.add("dve", lambda e: e.reciprocal(out=u32[:, 0, :], in_=ps[:, ob + 1, :]), reads=[PB[ob + 1]], writes=[R_u32[0]])
                    S.add("dve", lambda e: e.tensor_tensor(out=mixT[:, 8 + h, :], in0=ps[:, ob, :], in1=u32[:, 0, :], op=ALU.mult),
                          reads=[PB[ob], R_u32[0]], writes=[R_mix])

            for h in range(8):
                hb2 = h % 2
                gam = 1.0 - 2.0 ** (-5.0 - h)
                ld2(bq[:, hb2, :], c_bq[:, h, :], R_bq[hb2])
                ld2(dbd[:, hb2, :, :], c_dbd[:, h, :, :], R_bq[hb2])
                for c in range(4):
                    ld2(KTh[0:64, hb2, c, :], kva[c * KV_ROWS + 2112 + 64 * h:c * KV_ROWS + 2112 + 64 * (h + 1), :], R_KT[hb2], [R_kvall])
                    ld2(Vh[:, hb2, c, :, :], kvV(c)[:, 16 + h, :, :], R_V[hb2], [R_kvall])
                ob = 0 if hb2 == 0 else 4
                first = True
                for G in range(nG):
                    col0 = max(0, (G - 4 * t) * 128)
                    for c in range(4):
                        sbk = 2 + (sctr[0] % 2)
                        sctr[0] += 1
                        pk = pctr[0] % 3
                        pctr[0] += 1
                        S.add("pe", (lambda G_, c_, sbk_, col0_: (lambda e: e.matmul(
                            ps[:, sbk_, col0_:T], lhsT=KTh[0:64, hb2, c_, G_ * 128:(G_ + 1) * 128], rhs=rqT2[0:64, h, col0_:T],
                            start=True, stop=True)))(G, c, sbk, col0), reads=[R_KT[hb2], R_p2], writes=[PB[sbk]])
                        if G >= 4 * t:
                            S.add("dve", (lambda c_, sbk_, pk_, col0_: (lambda e: e.tensor_tensor(
                                out=pT[:, pk_, col0_:col0_ + 128], in0=ps[:, sbk_, col0_:col0_ + 128], in1=dbd[:, hb2, c_, :], op=ALU.mult)))(c, sbk, pk, col0),
                                reads=[PB[sbk], R_bq[hb2]], writes=[R_pT[pk]])
                            if col0 + 128 < T:
                                S.add("dve", (lambda sbk_, pk_, col0_: (lambda e: e.tensor_tensor(
                                    out=pT[:, pk_, col0_ + 128:T], in0=ps[:, sbk_, col0_ + 128:T], in1=bq[:, hb2, 128:T - col0_], op=ALU.mult)))(sbk, pk, col0),
                                    reads=[PB[sbk], R_bq[hb2]], writes=[R_pT[pk]])
                        else:
                            cst = float(gam ** (512.0 * (4 * t - G)))
                            S.add("dve", (lambda sbk_, pk_, cst_: (lambda e: e.scalar_tensor_tensor(
                                out=pT[:, pk_, :], in0=ps[:, sbk_, :], scalar=cst_, in1=bq[:, hb2, :], op0=ALU.mult, op1=ALU.mult)))(sbk, pk, cst),
                                reads=[PB[sbk], R_bq[hb2]], writes=[R_pT[pk]])
                        last = (G == nG - 1 and c == 3)
                        S.add("pe", (lambda G_, c_, pk_, col0_, first_, last_: (lambda e: e.matmul(
                            ps[:, ob, col0_:T], lhsT=Vh[:, hb2, c_, G_, :], rhs=pT[:, pk_, col0_:T], start=first_, stop=last_)))(G, c, pk, col0, first, last),
                            reads=[R_V[hb2], R_pT[pk]], writes=[PB[ob]])
                        first = False
                S.add("act", lambda e: e.activation(out=sq2[:], in_=ps[:, ob, :], func=AF.Square), reads=[PB[ob]], writes=[R_sq2])
                S.add("pe", lambda e: e.matmul(ps[:, ob + 1, :], lhsT=ones[:], rhs=sq2[:], start=True, stop=True),
                      reads=[R_sq2, R_const], writes=[PB[ob + 1]])
                S.add("act", lambda e: e.activation(out=u32[:, 0, :], in_=ps[:, ob + 1, :], func=AF.Sqrt, bias=eps_t[:, :], scale=1.0 / 128),
                      reads=[PB[ob + 1], R_const], writes=[R_u32[0]])
                S.add("dve", lambda e: e.reciprocal(out=u32[:, 1, :], in_=u32[:, 0, :]), reads=[R_u32[0]], writes=[R_u32[1]])
                S.add("dve", lambda e: e.tensor_tensor(out=u32[:, 0, :], in0=ps[:, ob, :], in1=u32[:, 1, :], op=ALU.mult),
                      reads=[PB[ob], R_u32[1]], writes=[R_u32[0]])
                S.add("dve", lambda e: e.tensor_tensor(out=mixT[:, 24 + h, :], in0=u32[:, 0, :], in1=gs[:, h, :], op=ALU.mult),
                      reads=[R_u32[0], R_p2], writes=[R_mix])
            if debug:
                o_ = S.add("act", lambda e: e.dma_start(out=dbg["mix"][t], in_=mixT[:]), reads=[R_mix], writes=[], dma=True)
                final_ops.append(o_)
            S.fence()
            sa.close()
            if stop == "p2attn":
                st.close()
                if t == NT - 1:
                    stopped = True
                continue

            sb2 = contextlib.ExitStack()
            sb2.__enter__()
            yst = sb("yst", [128, 2, 4, 512], F32, sb2)
            ssq = ssq_p
            junk = sb("junk", [128, 512], BF16, sb2)
            R_yst = [Res("yst0"), Res("yst1")]
            R_ssq = Res("ssq")
            R_junk = Res("junk")
            R_ys = Res("ys")
            wo = wall["w_out", l].ap().rearrange("(kc p) n -> p kc n", p=128)
            for cg in range(8):
                pb0 = 4
                for kt in range(4):
                    wv, wr = wload(("w_out", l), wo[:, kt * 8:(kt + 1) * 8, cg * 512:(cg + 1) * 512], 8, 512)
                    for kc in range(8):
                        k = kt * 8 + kc
                        for blk in range(4):
                            S.add("pe", (lambda wv_, kc_, k_, blk_: (lambda e: e.matmul(
                                ps[:, pb0 + blk_, :], lhsT=mixT[:, k_, blk_ * 128:(blk_ + 1) * 128], rhs=wv_[:, kc_, :],
                                start=(k_ == 0), stop=(k_ == 31))))(wv, kc, k, blk), reads=[wr, R_mix], writes=[PB[pb0 + blk]])
                y2 = cg % 2
                for blk in range(4):
                    S.add("act", (lambda blk_: (lambda e: e.activation(out=yst[:, y2, blk_, :], in_=ps[:, pb0 + blk_, :], func=AF.Copy)))(blk),
                          reads=[PB[pb0 + blk]], writes=[R_yst[y2]])
                    S.add("act", (lambda blk_: (lambda e: e.activation(out=junk[:], in_=yst[:, y2, blk_, :], func=AF.Square,
                                                                       accum_out=ssq[:, blk_, cg:cg + 1])))(blk),
                          reads=[R_yst[y2]], writes=[R_junk, R_ssq])
                S.add("act", lambda e: e.dma_start(out=yscr.ap()[t * T:(t + 1) * T, cg * 512:(cg + 1) * 512].rearrange("(b p) n -> p b n", p=128),
                                                   in_=yst[:, y2, :, :]), reads=[R_yst[y2]], writes=[R_ys], dma=True)
            S.fence()
            sb2.close()
            st.close()
            if stop == "p2out":
                if t == NT - 1:
                    final_ops.extend(big_copy(dbg["xm0"], yscr, TOK, [R_ys], step=128))
                    stopped = True
                continue

            st = contextlib.ExitStack()
            st.__enter__()
            facc = sb("facc", [128, 4, D], F32, st)
            sc_ = contextlib.ExitStack()
            sc_.__enter__()
            h2T = sb("h2T", [128, 32, T], BF16, sc_)
            R_h2T = Res("h2T")
            sd = contextlib.ExitStack()
            sd.__enter__()
            gbc = sb("gbc", [128, D], F32, sd)
            ybk = sb("ybk", [128, D], F32, sd)
            xin = sb("xin", [128, D], F32, sd)
            hb3 = sb("hb3", [128, D], BF16, sd)
            gffn = sb("gffn", [128, 32], F32, sd)
            s4 = sb("s4", [128, 16], F32, sd)
            R_gbc, R_ybk, R_xin, R_hb3, R_s4 = Res("gbc"), Res("ybk"), Res("xin"), Res("hb3"), Res("s4")
            S.add("act", lambda e: e.dma_start(out=gbc[:], in_=p_gpost[l].partition_broadcast(128)), writes=[R_gbc], dma=True)
            S.add("act", lambda e: e.dma_start(out=gffn[:], in_=p_gffn[l]), writes=[R_gbc], dma=True)
            S.add("dve", lambda e: e.tensor_reduce(out=s4[:, 0:4], in_=ssq[:, :, :], axis=AX.X, op=ALU.add), reads=[R_ssq], writes=[R_s4])
            S.add("act", lambda e: e.activation(out=s4[:, 4:8], in_=s4[:, 0:4], func=AF.Sqrt, bias=eps_t[:, :], scale=1.0 / D),
                  reads=[R_s4, R_const], writes=[R_s4])
            S.add("dve", lambda e: e.reciprocal(out=s4[:, 8:12], in_=s4[:, 4:8]), reads=[R_s4], writes=[R_s4])
            for blk in range(4):
                r0 = t * T + blk * 128
                S.add("act", (lambda r0_: (lambda e: e.dma_start(out=ybk[:], in_=yscr.ap()[r0_:r0_ + 128, :])))(r0), reads=[R_ys], writes=[R_ybk], dma=True)
                S.add("act", (lambda r0_: (lambda e: e.dma_start(out=xin[:], in_=x_src[r0_:r0_ + 128, :])))(r0), reads=[R_xsrc], writes=[R_xin], dma=True)
                S.add("act", (lambda blk_: (lambda e: e.activation(out=ybk[:], in_=ybk[:], func=AF.Copy, scale=s4[:, 8 + blk_:9 + blk_])))(blk),
                      reads=[R_ybk, R_s4], writes=[R_ybk])
                S.add("dve", lambda e: e.tensor_tensor(out=ybk[:], in0=ybk[:], in1=gbc[:], op=ALU.mult), reads=[R_ybk, R_gbc], writes=[R_ybk])
                S.add("dve", lambda e: e.tensor_tensor(out=xin[:], in0=xin[:], in1=ybk[:], op=ALU.add), reads=[R_ybk, R_xin], writes=[R_xin])
                S.add("act", (lambda r0_: (lambda e: e.dma_start(out=xm[r0_:r0_ + 128, :], in_=xin[:])))(r0), reads=[R_xin], writes=[R_xm], dma=True)
                S.add("act", (lambda blk_: (lambda e: e.activation(out=hb3[:], in_=xin[:], func=AF.Square, accum_out=s4[:, 12 + blk_:13 + blk_])))(blk),
                      reads=[R_xin], writes=[R_hb3, R_s4])
                S.add("act", (lambda blk_: (lambda e: e.activation(out=s4[:, 12 + blk_:13 + blk_], in_=s4[:, 12 + blk_:13 + blk_], func=AF.Sqrt,
                                                                   bias=eps_t[:, :], scale=1.0 / D)))(blk), reads=[R_s4, R_const], writes=[R_s4])
                S.add("dve", (lambda blk_: (lambda e: e.reciprocal(out=s4[:, 12 + blk_:13 + blk_], in_=s4[:, 12 + blk_:13 + blk_])))(blk),
                      reads=[R_s4], writes=[R_s4])
                S.add("act", (lambda blk_: (lambda e: e.activation(out=hb3[:], in_=xin[:], func=AF.Copy, scale=s4[:, 12 + blk_:13 + blk_])))(blk),
                      reads=[R_xin, R_s4], writes=[R_hb3])
                for cg in range(8):
                    bank = cg % 2
                    pv = ps[:, bank, :].bitcast(BF16)
                    for c4 in range(4):
                        c = cg * 4 + c4
                        S.add("pe", (lambda pv_, c4_, c_: (lambda e: e.transpose(
                            out=pv_[:, c4_ * 128:(c4_ + 1) * 128], in_=hb3[:, c_ * 128:(c_ + 1) * 128], identity=ident[:])))(pv, c4, c),
                            reads=[R_hb3, R_const], writes=[PB[bank]])
                    for c4 in range(4):
                        c = cg * 4 + c4
                        S.add("dve", (lambda pv_, c4_, c_, blk_: (lambda e: e.tensor_scalar(
                            out=h2T[:, c_, blk_ * 128:(blk_ + 1) * 128], in0=pv_[:, c4_ * 128:(c4_ + 1) * 128],
                            scalar1=gffn[:, c_:c_ + 1], scalar2=None, op0=ALU.mult)))(pv, c4, c, blk),
                            reads=[PB[bank], R_gbc], writes=[R_h2T])
            S.fence()
            sd.close()

            se = contextlib.ExitStack()
            se.__enter__()
            actT = sb("actT", [128, 2, 8, T], BF16, se)
            r32 = sb("r32", [128, 2, T], F32, se)
            R_act = [Res("act0"), Res("act1")]
            R_r32 = [Res("r32_0"), Res("r32_1")]
            R_facc = [Res("facc%d" % i) for i in range(4)]
            wu = wall["w_up", l].ap().rearrange("(kc p) n -> p kc n", p=128)
            wd = wall["w_down", l].ap().rearrange("(kc p) n -> p kc n", p=128)
            uctr = [0]
            for grp in range(16):
                a2 = grp % 2
                for pr in range(4):
                    f0 = grp * 1024 + pr * 256
                    ub = 2 * (uctr[0] % 2)
                    uctr[0] += 1
                    for kt in range(2):
                        wv, wr = wload(("w_up", l), wu[:, kt * 16:(kt + 1) * 16, f0:f0 + 256], 16, 256)
                        for kc in range(16):
                            k = kt * 16 + kc
                            for j in range(2):
                                S.add("pe", (lambda wv_, kc_, k_, j_, ub_: (lambda e: e.matmul(
                                    ps[:, ub_ + j_, :], lhsT=wv_[:, kc_, j_ * 128:(j_ + 1) * 128], rhs=h2T[:, k_, :],
                                    start=(k_ == 0), stop=(k_ == 31))))(wv, kc, k, j, ub), reads=[wr, R_h2T], writes=[PB[ub + j]])
                    for j in range(2):
                        S.add("act", (lambda j_, ub_: (lambda e: e.activation(out=r32[:, j_, :], in_=ps[:, ub_ + j_, :], func=AF.Relu)))(j, ub),
                              reads=[PB[ub + j]], writes=[R_r32[j]])
                        S.add("dve", (lambda j_, pr_: (lambda e: e.tensor_tensor(out=actT[:, a2, pr_ * 2 + j_, :], in0=r32[:, j_, :], in1=r32[:, j_, :], op=ALU.mult)))(j, pr),
                              reads=[R_r32[j]], writes=[R_act[a2]])
                for cg in range(8):
                    wv, wr = wload(("w_down", l), wd[:, grp * 8:(grp + 1) * 8, cg * 512:(cg + 1) * 512], 8, 512)
                    for half in range(2):
                        for kc in range(8):
                            for b2 in range(2):
                                blk = half * 2 + b2
                                S.add("pe", (lambda wv_, kc_, blk_: (lambda e: e.matmul(
                                    ps[:, 4 + blk_, :], lhsT=actT[:, a2, kc_, blk_ * 128:(blk_ + 1) * 128], rhs=wv_[:, kc_, :],
                                    start=(kc_ == 0), stop=(kc_ == 7))))(wv, kc, blk), reads=[wr, R_act[a2]], writes=[PB[4 + blk]])
                        for b2 in range(2):
                            blk = half * 2 + b2
                            if grp == 0:
                                S.add("dve", (lambda blk_, cg_: (lambda e: e.tensor_copy(out=facc[:, blk_, cg_ * 512:(cg_ + 1) * 512], in_=ps[:, 4 + blk_, :])))(blk, cg),
                                      reads=[PB[4 + blk]], writes=[R_facc[blk]])
                            else:
                                S.add("dve", (lambda blk_, cg_: (lambda e: e.tensor_tensor(out=facc[:, blk_, cg_ * 512:(cg_ + 1) * 512],
                                                                                           in0=ps[:, 4 + blk_, :], in1=facc[:, blk_, cg_ * 512:(cg_ + 1) * 512], op=ALU.add)))(blk, cg),
                                      reads=[PB[4 + blk], R_facc[blk]], writes=[R_facc[blk]])
            S.fence()
            se.close()
            sc_.close()

            sf = contextlib.ExitStack()
            sf.__enter__()
            gbc2 = sb("gbc2", [128, D], F32, sf)
            xin2 = sb("xin2", [128, 2, D], F32, sf)
            junk2 = sb("junk2", [128, D], BF16, sf)
            s5 = sb("s5", [128, 16], F32, sf)
            R_g2, R_s5, R_j2 = Res("gbc2"), Res("s5"), Res("junk2")
            R_xin2 = [Res("xin2_0"), Res("xin2_1")]
            S.add("act", lambda e: e.dma_start(out=gbc2[:], in_=p_gfpost[l].partition_broadcast(128)), writes=[R_g2], dma=True)
            for blk in range(4):
                r0 = t * T + blk * 128
                x2 = blk % 2
                S.add("act", (lambda r0_, x2_: (lambda e: e.dma_start(out=xin2[:, x2_, :], in_=xm[r0_:r0_ + 128, :])))(r0, x2),
                      reads=[R_xm], writes=[R_xin2[x2]], dma=True)
                S.add("act", (lambda blk_: (lambda e: e.activation(out=junk2[:], in_=facc[:, blk_, :], func=AF.Square, accum_out=s5[:, blk_:blk_ + 1])))(blk),
                      reads=[R_facc[blk]], writes=[R_j2, R_s5])
                S.add("act", (lambda blk_: (lambda e: e.activation(out=s5[:, 4 + blk_:5 + blk_], in_=s5[:, blk_:blk_ + 1], func=AF.Sqrt,
                                                                   bias=eps_t[:, :], scale=1.0 / D)))(blk), reads=[R_s5, R_const], writes=[R_s5])
                S.add("dve", (lambda blk_: (lambda e: e.reciprocal(out=s5[:, 8 + blk_:9 + blk_], in_=s5[:, 4 + blk_:5 + blk_])))(blk), reads=[R_s5], writes=[R_s5])
                S.add("act", (lambda blk_: (lambda e: e.activation(out=facc[:, blk_, :], in_=facc[:, blk_, :], func=AF.Copy, scale=s5[:, 8 + blk_:9 + blk_])))(blk),
                      reads=[R_facc[blk], R_s5], writes=[R_facc[blk]])
                S.add("dve", (lambda blk_: (lambda e: e.tensor_tensor(out=facc[:, blk_, :], in0=facc[:, blk_, :], in1=gbc2[:], op=ALU.mult)))(blk),
                      reads=[R_facc[blk], R_g2], writes=[R_facc[blk]])
                S.add("dve", (lambda blk_, x2_: (lambda e: e.tensor_tensor(out=facc[:, blk_, :], in0=facc[:, blk_, :], in1=xin2[:, x2_, :], op=ALU.add)))(blk, x2),
                      reads=[R_facc[blk], R_xin2[x2]], writes=[R_facc[blk]])
                o_ = S.add("act", (lambda r0_, blk_: (lambda e: e.dma_start(out=x_dst[r0_:r0_ + 128, :], in_=facc[:, blk_, :])))(r0, blk),
                           reads=[R_facc[blk]], writes=[R_xdst], dma=True)
                final_ops.append(o_)
                if debug and l == 0:
                    o2 = S.add("act", (lambda r0_, blk_: (lambda e: e.dma_start(out=dbg["x1"][r0_:r0_ + 128, :], in_=facc[:, blk_, :])))(r0, blk),
                               reads=[R_facc[blk]], writes=[], dma=True)
                    final_ops.append(o2)
            S.fence()
            sf.close()
            st.close()
        if stopped:
            break
        if debug and l == 0:
            final_ops.extend(big_copy(dbg["xm0"], xm, TOK, [R_xm], step=128))
        S.fence()
        if stop == "full1":
            break
    S.emit(final_ops)
    top.close()
    nc._used_w = list(wsh.keys())
    return nc


def _tables(k):
    n = np.arange(TOK)
    pos = ((4 * (n // 128) + k) * 128 + (n % 128)).astype(np.float32)
    inv_freq = (1.0 / (np.float32(10000.0) ** (np.arange(0, 64, 2, dtype=np.float32) / np.float32(64)))).astype(np.float32)
    ang = (pos[:, None] * inv_freq[None, :]).astype(np.float32)
    cos, sin = np.cos(ang).astype(np.float32), np.sin(ang).astype(np.float32)
    rope = np.zeros((128, 2, TOK), np.float32)
    for f in range(128):
        ff = f % 64
        rope[f, 0] = cos[:, ff % 32]
        rope[f, 1] = -sin[:, ff % 32] if ff < 32 else sin[:, ff % 32]
    NEG = -30000.0
    bias = np.zeros((128, 8), np.float32)
    for c in range(4):
        bias[0:64, c] = 0.0 if c <= k else NEG
        bias[64:128, c] = 0.0 if c < k else NEG
        bias[:, 4 + c] = 0.0 if c <= k else NEG
    gam = 1.0 - 2.0 ** (-5.0 - np.arange(8, dtype=np.float64))
    rsc = np.zeros((128, 8), np.float64)
    for j in range(4):
        for half in range(2):
            h = 2 * j + half
            rsc[half * 64:(half + 1) * 64, j] = RET_SCALE * gam[h] ** (128.0 * k)
            rsc[half * 64:(half + 1) * 64, 4 + j] = gam[h] ** (-128.0 * k)
    jj = np.arange(128)[:, None].astype(np.float64)
    ii = np.arange(128)[None, :].astype(np.float64)
    dbd = np.zeros((128, 8, 4, 128), np.float64)
    bqt = np.zeros((128, 8, 512), np.float64)
    for h in range(8):
        for c in range(4):
            if c < k:
                dbd[:, h, c, :] = gam[h] ** (ii - jj)
            elif c == k:
                vis = (np.floor(jj / 64) <= np.floor(ii / 64))
                dbd[:, h, c, :] = np.where(vis, gam[h] ** np.abs(ii - jj), 0.0)
        for m in range(4):
            bqt[:, h, m * 128:(m + 1) * 128] = gam[h] ** (512.0 * m + ii - jj)
    return rope, bias, rsc.astype(np.float32), dbd.astype(np.float32), bqt.astype(np.float32)


_PROG = {}


def _consts():
    ident = np.eye(128, dtype=np.float32).astype(ml_dtypes.bfloat16)
    swap = np.zeros((128, 128), np.float32)
    for m in range(128):
        swap[64 * (m // 64) + ((m % 64) + 32) % 64, m] = 1.0
    return ident, swap.astype(ml_dtypes.bfloat16)


def kernel(x, norm_mix_pre, norm_mix_post, norm_ffn_pre, norm_ffn_post, w_in,
           sgu_ln_g, sgu_ln_b, sgu_w, sgu_b, mla_q_norm, mla_wq_b, mla_kv_norm, mla_wkv_b,
           w_out, w_up, w_down, _debug=False, _stop=None):
    f32 = lambda a: np.ascontiguousarray(np.asarray(a, dtype=np.float32))
    x = f32(x)
    weights = {"w_in": f32(w_in), "wq_b": f32(mla_wq_b), "wkv_b": f32(mla_wkv_b),
               "w_out": f32(w_out), "w_up": f32(w_up), "w_down": f32(w_down)}
    key = (bool(_debug), _stop)
    if key not in _PROG:
        _PROG[key] = build_program(debug=key[0], stop=_stop)
    nc = _PROG[key]
    ident, swap = _consts()

    def pc(a, nch):
        a = f32(a)
        return np.ascontiguousarray(a.reshape(a.shape[0], nch, 128).transpose(0, 2, 1))

    common = {
        "p_gpre": pc(norm_mix_pre, 32), "p_gffn": pc(norm_ffn_pre, 32),
        "p_gq": pc(mla_q_norm, 8), "p_gkv": pc(mla_kv_norm, 4),
        "p_gpost": f32(norm_mix_post), "p_gfpost": f32(norm_ffn_post),
        "p_lng": f32(sgu_ln_g), "p_lnb": f32(sgu_ln_b),
        "p_sgub": f32(sgu_b).reshape(DEPTH, 1024),
        "p_sguwT": np.ascontiguousarray(f32(sgu_w).transpose(0, 3, 1, 2)),
        "c_ident": ident, "c_swap": swap,
    }
    xb = x.reshape(2, 32, 128, D)
    in_maps = []
    for c in range(NCORES):
        b, k = c // 4, c % 4
        m = dict(common)
        m["x"] = np.ascontiguousarray(xb[b, k::4].reshape(TOK, D))
        for l in range(DEPTH):
            for name, K, N in WSPEC:
                if (name, l) not in nc._used_w:
                    continue
                r = K // 8
                m["%s%d_sh" % (name, l)] = np.ascontiguousarray(weights[name][l, c * r:(c + 1) * r, :])
        rope, bias, rsc, dbd, bqt = _tables(k)
        m["c_rope"], m["c_bias"], m["c_rsc"], m["c_dbd"], m["c_bq"] = rope, bias, rsc, dbd, bqt
        selv = np.zeros((128, 2), np.float32)
        selv[:, b] = 1.0
        m["c_sel"] = selv
        in_maps.append(m)
    res = run_bass_kernel_spmd(nc, in_maps, core_ids=list(range(NCORES)))
    out = np.zeros((2, 32, 128, D), np.float32)
    for c in range(NCORES):
        b, k = c // 4, c % 4
        out[b, k::4] = np.asarray(res.results[c]["out"], dtype=np.float32).reshape(8, 128, D)
    if _debug:
        kernel._dbg = [{n: np.asarray(v) for n, v in r.items()} for r in res.results]
    return out.reshape(2, SEQ, D)
```
